# Optimizing a Trainium2 kernel written in Bass

```python
import math
import jax
import jax.numpy as jnp
from jax import lax
import numpy as np

D_MODEL = 4096
BATCH = 2
SEQ = 4096
DEPTH = 1
DEC_BATCH = 8
DEC_SEQ = 64
PAST_LEN = 1024

CHUNK = 64
WINDOW = 128
WINDOW_CHUNKS = WINDOW // CHUNK
HEAD_DIM = 64
ATTN_WIDTH = D_MODEL // 2
N_Q_HEADS = ATTN_WIDTH // HEAD_DIM
N_KV_HEADS = max(1, N_Q_HEADS // 8)
GQA_REP = N_Q_HEADS // N_KV_HEADS
KV_WIDTH = N_KV_HEADS * HEAD_DIM
SSM_WIDTH = D_MODEL // 2
SSM_GROUP = 16
N_GROUPS = SSM_WIDTH // SSM_GROUP
STATE_DIM = 64
ROPE_THETA = 10000.0
NORM_EPS = 1e-5
DT_MIN = 1e-3
DT_MAX = 1e-1
IN_WIDTH = 2 * ATTN_WIDTH + 2 * KV_WIDTH + 2 * SSM_WIDTH + 2 * D_MODEL

kernel_name = 'chunk_streaming_swa_sink_s5_hybrid_step'


def _rmsnorm(x, g):
    xf = x.astype(jnp.float32)
    y = xf * lax.rsqrt(jnp.mean(xf * xf, axis=-1, keepdims=True) + NORM_EPS)
    return (y * g.astype(jnp.float32)).astype(x.dtype)


def _rope(x, pos):
    half = HEAD_DIM // 2
    inv_freq = ROPE_THETA ** (-jnp.arange(half, dtype=jnp.float32) / half)
    ang = pos.astype(jnp.float32)[:, None] * inv_freq[None, :]
    cos = jnp.cos(ang)[None, :, None, :]
    sin = jnp.sin(ang)[None, :, None, :]
    xf = x.astype(jnp.float32)
    x1, x2 = xf[..., :half], xf[..., half:]
    return jnp.concatenate([x1 * cos - x2 * sin, x2 * cos + x1 * sin], axis=-1).astype(x.dtype)


def _attend(q, k, v, sink, mask):
    s = jnp.einsum('...qgrd,...kgd->...grqk', q, k, preferred_element_type=jnp.float32)
    s = s * (HEAD_DIM ** -0.5)
    if mask is not None:
        s = jnp.where(mask, s, -jnp.inf)
    sk = jnp.broadcast_to(sink.astype(jnp.float32)[:, :, None, None], s.shape[:-1] + (1,))
    p = jax.nn.softmax(jnp.concatenate([s, sk], axis=-1), axis=-1)[..., :-1]
    return jnp.einsum('...grqk,...kgd->...qgrd', p.astype(v.dtype), v)


def _swa_prompt(q, k, v, sink):
    b, t = q.shape[0], q.shape[1]
    nc = t // CHUNK
    qc = q.reshape(b, nc, CHUNK, N_KV_HEADS, GQA_REP, HEAD_DIM)
    kc = k.reshape(b, nc, CHUNK, N_KV_HEADS, HEAD_DIM)
    vc = v.reshape(b, nc, CHUNK, N_KV_HEADS, HEAD_DIM)
    pad = ((0, 0), (WINDOW_CHUNKS, 0), (0, 0), (0, 0), (0, 0))
    kp = jnp.pad(kc, pad)
    vp = jnp.pad(vc, pad)
    kb = jnp.concatenate([kp[:, j:j + nc] for j in range(WINDOW_CHUNKS + 1)], axis=2)
    vb = jnp.concatenate([vp[:, j:j + nc] for j in range(WINDOW_CHUNKS + 1)], axis=2)
    key_chunk = (jnp.arange(nc)[:, None] - WINDOW_CHUNKS
                 + (jnp.arange((WINDOW_CHUNKS + 1) * CHUNK) // CHUNK)[None, :])
    mask = (key_chunk >= 0)[:, None, None, None, :]
    o = _attend(qc, kb, vb, sink, mask)
    return o.reshape(b, t, ATTN_WIDTH)


def _swa_cached(q, k, v, cache_k, cache_v, sink):
    b, t = q.shape[0], q.shape[1]
    kb = jnp.concatenate([cache_k.astype(k.dtype), k], axis=1)
    vb = jnp.concatenate([cache_v.astype(v.dtype), v], axis=1)
    qg = q.reshape(b, t, N_KV_HEADS, GQA_REP, HEAD_DIM)
    o = _attend(qg, kb, vb, sink, None)
    return o.reshape(b, t, ATTN_WIDTH)


def _s5_discretise(lambda_re, lambda_im, log_dt, b_re, b_im):
    lr = jnp.minimum(lambda_re.astype(jnp.float32), -1e-4)
    li = lambda_im.astype(jnp.float32)
    dt = jnp.exp(log_dt.astype(jnp.float32))[:, None]
    mag = jnp.exp(lr * dt)
    a_re = mag * jnp.cos(li * dt)
    a_im = mag * jnp.sin(li * dt)
    den = lr * lr + li * li
    nr = a_re - 1.0
    f_re = (nr * lr + a_im * li) / den
    f_im = (a_im * lr - nr * li) / den
    br = b_re.astype(jnp.float32)
    bi = b_im.astype(jnp.float32)
    bb_re = f_re[..., None] * br - f_im[..., None] * bi
    bb_im = f_re[..., None] * bi + f_im[..., None] * br
    return a_re, a_im, bb_re, bb_im


def _cmul_combine(left, right):
    al_re, al_im, bl_re, bl_im = left
    ar_re, ar_im, br_re, br_im = right
    return (al_re * ar_re - al_im * ar_im,
            al_re * ar_im + al_im * ar_re,
            ar_re * bl_re - ar_im * bl_im + br_re,
            ar_re * bl_im + ar_im * bl_re + br_im)


def _s5_scan(u, h0_re, h0_im, a_re, a_im, bb_re, bb_im, c_re, c_im, d_skip):
    b, t = u.shape[0], u.shape[1]
    uf = u.astype(jnp.float32).reshape(b, t, N_GROUPS, SSM_GROUP)
    x_re = jnp.einsum('btgi,gpi->btgp', uf, bb_re)
    x_im = jnp.einsum('btgi,gpi->btgp', uf, bb_im)
    h0r = h0_re.astype(jnp.float32)
    h0i = h0_im.astype(jnp.float32)
    x_re = x_re.at[:, 0].add(a_re * h0r - a_im * h0i)
    x_im = x_im.at[:, 0].add(a_re * h0i + a_im * h0r)
    a_re_t = jnp.broadcast_to(a_re, (1, t) + a_re.shape)
    a_im_t = jnp.broadcast_to(a_im, (1, t) + a_im.shape)
    _, _, h_re, h_im = lax.associative_scan(_cmul_combine, (a_re_t, a_im_t, x_re, x_im), axis=1)
    y = (jnp.einsum('btgp,gip->btgi', h_re, c_re.astype(jnp.float32))
         - jnp.einsum('btgp,gip->btgi', h_im, c_im.astype(jnp.float32))
         + d_skip.astype(jnp.float32) * uf)
    return y.reshape(b, t, SSM_WIDTH).astype(u.dtype), h_re[:, -1], h_im[:, -1]


def _layer(x, pos, kv_cache_k, kv_cache_v, h0_re, h0_im, disc, norm_g, w_in, sink,
           c_re, c_im, d_skip, w_glu, b_glu, w_pa, w_ps, w_out):
    b, t, _ = x.shape
    h = _rmsnorm(x, norm_g)
    proj = jnp.einsum('btd,de->bte', h, w_in)
    sizes = (ATTN_WIDTH, KV_WIDTH, KV_WIDTH, ATTN_WIDTH, SSM_WIDTH, SSM_WIDTH, D_MODEL, D_MODEL)
    points = [int(p) for p in np.cumsum(sizes)[:-1]]
    q, k, v, z_a, u, z_s, g_a, g_s = jnp.split(proj, points, axis=-1)
    q = _rope(q.reshape(b, t, N_Q_HEADS, HEAD_DIM), pos)
    k = _rope(k.reshape(b, t, N_KV_HEADS, HEAD_DIM), pos)
    v = v.reshape(b, t, N_KV_HEADS, HEAD_DIM)
    sink_g = sink.reshape(N_KV_HEADS, GQA_REP)
    if kv_cache_k is None:
        attn = _swa_prompt(q, k, v, sink_g)
    else:
        attn = _swa_cached(q, k, v, kv_cache_k, kv_cache_v, sink_g)
    a_re, a_im, bb_re, bb_im = disc
    y_ssm, hl_re, hl_im = _s5_scan(u, h0_re, h0_im, a_re, a_im, bb_re, bb_im, c_re, c_im, d_skip)
    glu = jnp.einsum('btc,ce->bte', y_ssm, w_glu) + b_glu
    glu_a, glu_g = jnp.split(glu, 2, axis=-1)
    ssm_out = glu_a * jax.nn.sigmoid(glu_g)
    br_a = jnp.einsum('btc,cd->btd', attn * jax.nn.silu(z_a), w_pa)
    br_s = jnp.einsum('btc,cd->btd', ssm_out * jax.nn.silu(z_s), w_ps)
    merged = jax.nn.sigmoid(g_a) * br_a + jax.nn.sigmoid(g_s) * br_s
    out = x + jnp.einsum('btd,de->bte', merged, w_out)
    return out, k, v, hl_re, hl_im


def setup_inputs(seed: int = 0) -> dict:
    key = jax.random.key(seed)
    ks = jax.random.split(key, 23)
    f32 = jnp.float32
    rows = min(WINDOW, PAST_LEN)

    def nrm(k, shape, scale):
        return scale * jax.random.normal(k, shape, f32)

    return {
        'x_prompt': nrm(ks[0], (BATCH, SEQ, D_MODEL), 1.0),
        'x_sample': nrm(ks[1], (DEC_BATCH, DEC_SEQ, D_MODEL), 1.0),
        'cache_k': nrm(ks[2], (DEPTH, DEC_BATCH, rows, N_KV_HEADS, HEAD_DIM), 1.0),
        'cache_v': nrm(ks[3], (DEPTH, DEC_BATCH, rows, N_KV_HEADS, HEAD_DIM), 1.0),
        'state_ssm_re': nrm(ks[4], (DEPTH, DEC_BATCH, N_GROUPS, STATE_DIM), 0.3),
        'state_ssm_im': nrm(ks[5], (DEPTH, DEC_BATCH, N_GROUPS, STATE_DIM), 0.3),
        'norm_g': 1.0 + nrm(ks[6], (DEPTH, D_MODEL), 0.02),
        'w_in': nrm(ks[7], (DEPTH, D_MODEL, IN_WIDTH), D_MODEL ** -0.5),
        'sink': nrm(ks[8], (DEPTH, N_Q_HEADS), 1.0),
        'lambda_re': -0.5 + nrm(ks[9], (DEPTH, N_GROUPS, STATE_DIM), 0.01),
        'lambda_im': math.pi * jnp.arange(STATE_DIM, dtype=f32) + nrm(ks[10], (DEPTH, N_GROUPS, STATE_DIM), 0.01),
        'log_dt': jax.random.uniform(ks[11], (DEPTH, N_GROUPS), f32, math.log(DT_MIN), math.log(DT_MAX)),
        'b_re': nrm(ks[12], (DEPTH, N_GROUPS, STATE_DIM, SSM_GROUP), (2 * SSM_GROUP) ** -0.5),
        'b_im': nrm(ks[13], (DEPTH, N_GROUPS, STATE_DIM, SSM_GROUP), (2 * SSM_GROUP) ** -0.5),
        'c_re': nrm(ks[14], (DEPTH, N_GROUPS, SSM_GROUP, STATE_DIM), STATE_DIM ** -0.5),
        'c_im': nrm(ks[15], (DEPTH, N_GROUPS, SSM_GROUP, STATE_DIM), STATE_DIM ** -0.5),
        'd_skip': nrm(ks[16], (DEPTH, N_GROUPS, SSM_GROUP), 1.0),
        'w_glu': nrm(ks[17], (DEPTH, SSM_WIDTH, 2 * SSM_WIDTH), SSM_WIDTH ** -0.5),
        'b_glu': nrm(ks[18], (DEPTH, 2 * SSM_WIDTH), 0.01),
        'w_pa': nrm(ks[19], (DEPTH, ATTN_WIDTH, D_MODEL), ATTN_WIDTH ** -0.5),
        'w_ps': nrm(ks[20], (DEPTH, SSM_WIDTH, D_MODEL), SSM_WIDTH ** -0.5),
        'w_out': nrm(ks[21], (DEPTH, D_MODEL, D_MODEL), D_MODEL ** -0.5),
        'final_g': 1.0 + nrm(ks[22], (D_MODEL,), 0.02),
    }


def reference(x_prompt, x_sample, cache_k, cache_v, state_ssm_re, state_ssm_im,
              norm_g, w_in, sink, lambda_re, lambda_im, log_dt, b_re, b_im, c_re, c_im,
              d_skip, w_glu, b_glu, w_pa, w_ps, w_out, final_g):
    bp, sp = x_prompt.shape[0], x_prompt.shape[1]
    ts = x_sample.shape[1]
    pos_p = jnp.arange(sp, dtype=jnp.int32)
    pos_s = PAST_LEN + jnp.arange(ts, dtype=jnp.int32)
    zeros = jnp.zeros((bp, N_GROUPS, STATE_DIM), jnp.float32)
    keep = min(WINDOW, sp)
    xp, xs = x_prompt, x_sample
    kp_l, vp_l, hpr_l, hpi_l = [], [], [], []
    ks_l, vs_l, hsr_l, hsi_l = [], [], [], []
    for l in range(DEPTH):
        disc = _s5_discretise(lambda_re[l], lambda_im[l], log_dt[l], b_re[l], b_im[l])
        shared = (norm_g[l], w_in[l], sink[l], c_re[l], c_im[l], d_skip[l],
                  w_glu[l], b_glu[l], w_pa[l], w_ps[l], w_out[l])
        xp, kp, vp, hpr, hpi = _layer(xp, pos_p, None, None, zeros, zeros, disc, *shared)
        xs, kn, vn, hsr, hsi = _layer(xs, pos_s, cache_k[l], cache_v[l],
                                      state_ssm_re[l], state_ssm_im[l], disc, *shared)
        kp_l.append(kp[:, sp - keep:])
        vp_l.append(vp[:, sp - keep:])
        hpr_l.append(hpr)
        hpi_l.append(hpi)
        ks_l.append(kn)
        vs_l.append(vn)
        hsr_l.append(hsr)
        hsi_l.append(hsi)
    y_prompt = _rmsnorm(xp, final_g)
    y_sample = _rmsnorm(xs, final_g)
    return (y_prompt, y_sample,
            jnp.stack(kp_l), jnp.stack(vp_l), jnp.stack(hpr_l), jnp.stack(hpi_l),
            jnp.stack(ks_l), jnp.stack(vs_l), jnp.stack(hsr_l), jnp.stack(hsi_l))
```

```python
import math
from contextlib import ExitStack
import numpy as np
import concourse.bass as bass
import concourse.mybir as mybir
from concourse.bass_utils import run_bass_kernel_spmd

F32 = mybir.dt.float32
BF16 = mybir.dt.bfloat16
I32 = mybir.dt.int32
AF = mybir.ActivationFunctionType
ALU = mybir.AluOpType

NCORES = 8
D = 4096
TM = 1088
NCH = 136
PIECES = [(0, 384), (384, 384), (768, 320)]
TWO_PI = float(2 * np.pi)
PI = float(np.pi)
NEG = -30000.0

C_Q, C_K, C_V, C_ZA, C_U, C_ZS, C_GA, C_GS = 0, 2048, 2304, 2560, 4608, 6656, 8704, 12800

ARENA_BYTES = 180032


class Rec:
    ENG = ('pe', 'act', 'dve', 'pool', 'sp')

    def __init__(self):
        self.q = {e: [] for e in self.ENG}
        self.cnt = {e: 0 for e in self.ENG}
        self.dcnt = {}
        self.waited = {}
        self.hist = {}

    @staticmethod
    def hull(ap):
        pst, pc = ap.ap[0]
        off = ap.offset
        p0 = off // pst if pst else 0
        f0 = off % pst if pst else off
        esz = 4 if ap.dtype in (F32, I32) else 2
        span = sum((c - 1) * abs(st) for st, c in ap.ap[1:])
        return (ap.tensor.name, p0, p0 + pc, f0 * esz, (f0 + span + 1) * esz)

    @staticmethod
    def ovl(a, b):
        return a[0] == b[0] and a[1] < b[2] and b[1] < a[2] and a[3] < b[4] and b[3] < a[4]

    def op(self, eng, fn, r=(), w=()):
        if eng in ('act', 'dve', 'pool'):
            hist = self.hist.setdefault(eng, [])
            rh = [self.hull(x) for x in r if hasattr(x, 'ap')]
            wh = [self.hull(x) for x in w if hasattr(x, 'ap')]
            dep = None
            for (item, pw, pr) in reversed(hist[-8:]):
                if any(self.ovl(a, b) for a in pw for b in rh + wh):
                    dep = item
                    break
            if dep is None:
                for (item, pw, pr) in reversed(hist[-3:]):
                    if any(self.ovl(a, b) for a in pr for b in wh):
                        dep = item
                        break
            if dep is not None:
                self.q[eng].append(['drain'])
                del hist[:]
            item = ['op', fn, None, None]
            self.q[eng].append(item)
            hist.append((item, wh, rh))
            if len(hist) > 16:
                del hist[:8]
            return
        self.q[eng].append(['op', fn, None, None])

    def sig(self, eng):
        it = self.q[eng][-1]
        assert it[0] == 'op' and it[3] is None, (eng, it)
        if it[2] is None:
            self.cnt[eng] += 1
            it[2] = self.cnt[eng]
        return (eng, it[2])

    def dma(self, eng, fn, key):
        self.dcnt[key] = self.dcnt.get(key, 0) + 16
        self.q[eng].append(['op', fn, None, key])
        return ('dma:' + key, self.dcnt[key])

    def wait(self, eng, *toks):
        for tok in toks:
            if tok is None:
                continue
            if isinstance(tok, list):
                self.wait(eng, *tok)
                continue
            if tok[0] == eng:
                continue
            k = (eng, tok[0])
            if self.waited.get(k, 0) >= tok[1]:
                continue
            self.waited[k] = tok[1]
            self.q[eng].append(['wait', tok])


def build_program(dbg=None, stop=None, nocc=False):
    nc = bass.Bass("TRN2", target_bir_lowering=False)
    R = Rec()

    def din(name, shape, dt=F32):
        return nc.dram_tensor(name, list(shape), dt, kind="ExternalInput").ap()

    def dout(name, shape, dt=F32):
        return nc.dram_tensor(name, list(shape), dt, kind="ExternalOutput").ap()

    def dint(name, shape, dt=F32):
        return nc.dram_tensor(name, list(shape), dt, kind="Internal", addr_space="Local").ap()

    xs = din("xs", [1216, D])
    posv = din("posv", [1, 1216])
    invf2 = din("invf2", [128, 1])
    sgn = din("sgn", [128, 1])
    biasv_d = din("biasv", [128, 8])
    mcoef_d = din("mcoef", [128, 24])
    ck_d = din("ck", [128, 256])
    cv_d = din("cv", [128, 256])
    s0re_d = din("s0re", [128, 64])
    s0im_d = din("s0im", [128, 64])
    norm_g = din("norm_g", [1, D])
    final_g = din("final_g", [1, D])
    w_in = din("w_in", [D, 16896])
    sink_d = din("sink", [1, 32])
    lre_d = din("lre", [128, 64])
    lim_d = din("lim", [128, 64])
    ldt_d = din("ldt", [128, 1])
    bre_d = din("bre", [128, 1024])
    bim_d = din("bim", [128, 1024])
    cre_d = din("cre", [128, 1024])
    cim_d = din("cim", [128, 1024])
    dsk_d = din("dsk", [128, 16])
    w_glu = din("w_glu", [2048, 4096])
    b_glu = din("b_glu", [4096, 1])
    w_pa = din("w_pa", [2048, 4096])
    w_ps = din("w_ps", [2048, 4096])
    w_out = din("w_out", [4096, 4096])
    identf_d = din("identf", [128, 128])
    permT_d = din("permT", [128, 128])
    selE_d = din("selE", [128, 128])
    selO_d = din("selO", [128, 128])
    cmask_d = din("cmask", [128, 128])
    dmask_d = din("dmask", [128, 128])

    y_o = dout("y", [TM, D])
    kp_o = dout("kp", [128, 256])
    vp_o = dout("vp", [128, 256])
    ks_o = dout("ks", [64, 256])
    vs_o = dout("vs", [64, 256])
    sp_o = dout("ssmp", [128, 128])
    ss_o = dout("ssms", [128, 128])
    dbg_o = dout("dbg", list(dbg[1]), dbg[2]) if dbg else None

    U_dram = dint("U_dram", [128, 16, 8, NCH], BF16)
    Y_dram = dint("Y_dram", [128, 16, 8, NCH], BF16)
    szs_dram = dint("szs_dram", [16, 128, TM], BF16)
    gate_dram = dint("gate_dram", [64, 128, TM], BF16)
    WZ_d = dint("WZ_d", [128, 128, 128])
    WZP_d = dint("WZP_d", [128, 128, 128])
    R_d = dint("R_d", [128, 128, 128])
    WT1_d = dint("WT1_d", [128, 128, 128])
    T2_d = dint("T2_d", [128, 128, 128])
    coef_d = dint("coef_d", [128, 18, 128])
    cc_in = dint("cc_in", [128, 256])
    cc_out = dint("cc_out", [1024, 256])

    es = ExitStack()
    arena_t = es.enter_context(nc.sbuf_tensor("arena", [128, ARENA_BYTES // 4], F32))
    psb = [es.enter_context(nc.psum_tensor(f"ps{i}", [128, 512], F32)) for i in range(8)]
    wb_t = [es.enter_context(nc.sbuf_tensor(f"wbt{i}", [128, 32, 256], BF16)) for i in range(2)]

    def reg(off, shape, dt=F32):
        n = int(np.prod(shape))
        esz = 4 if dt in (F32, I32) else 2
        assert off % 4 == 0 and off + n * esz <= ARENA_BYTES, (off, shape)
        a = arena_t[:, off // 4: off // 4 + (n * esz + 3) // 4]
        if dt != F32:
            a = a.bitcast(dt)
            a = a[:, 0:n]
        if len(shape) == 2:
            a = a.rearrange("p (a b) -> p a b", a=shape[0])
        elif len(shape) == 3:
            a = a.rearrange("p (a b c) -> p a b c", a=shape[0], b=shape[1])
        elif len(shape) == 4:
            a = a.rearrange("p (a b c d) -> p a b c d", a=shape[0], b=shape[1], c=shape[2])
        return a

    def mm(out, lhsT, rhs, start=True, stop=True):
        R.op('pe', lambda e: e.matmul(out, lhsT=lhsT, rhs=rhs, start=start, stop=stop))

    def tr(out, in_, ident):
        R.op('pe', lambda e: e.transpose(out, in_, ident))

    def act(out, in_, func, bias=None, scale=None, accum=None, eng='act'):
        kw = {}
        if bias is not None:
            kw['bias'] = bias
        if scale is not None:
            kw['scale'] = scale
        if accum is not None:
            kw['accum_out'] = accum
        R.op('act', lambda e: e.activation(out=out, in_=in_, func=func, **kw), r=[in_, bias, scale], w=[out, accum])

    def tt(eng, out, a, b, op):
        R.op(eng, lambda e: e.tensor_tensor(out=out, in0=a, in1=b, op=op), r=[a, b], w=[out])

    def ts(eng, out, a, s1, s2, op0, op1=None):
        if op1 is None:
            R.op(eng, lambda e: e.tensor_scalar(out=out, in0=a, scalar1=s1, scalar2=None, op0=op0), r=[a, s1], w=[out])
        else:
            R.op(eng, lambda e: e.tensor_scalar(out=out, in0=a, scalar1=s1, scalar2=s2, op0=op0, op1=op1), r=[a, s1, s2], w=[out])

    def stt(out, a, scalar, b, op0, op1):
        R.op('dve', lambda e: e.scalar_tensor_tensor(out=out, in0=a, scalar=scalar, in1=b, op0=op0, op1=op1), r=[a, scalar, b], w=[out])

    def cp(eng, out, in_):
        if eng == 'act':
            R.op('act', lambda e: e.copy(out=out, in_=in_), r=[in_], w=[out])
        else:
            R.op(eng, lambda e: e.tensor_copy(out=out, in_=in_), r=[in_], w=[out])

    def mset(eng, ap, val):
        R.op(eng, lambda e: e.memset(ap, val), w=[ap])

    def dma(q, out, in_, key, nonc=False):
        if nonc:
            return R.dma(q, lambda e: e.dma_start(out=out, in_=in_, allow_slow_non_contiguous=True), key)
        return R.dma(q, lambda e: e.dma_start(out=out, in_=in_), key)

    def sig_last(eng):
        for it in reversed(R.q[eng]):
            if it[0] == 'op' and it[3] is None:
                if it[2] is None:
                    R.cnt[eng] += 1
                    it[2] = R.cnt[eng]
                return (eng, it[2])
        return None

    def barrier():
        toks = [sig_last(e) for e in ('pe', 'act', 'dve', 'pool')]
        toks += [('dma:' + k_, v_) for k_, v_ in R.dcnt.items()]
        for e in Rec.ENG:
            R.wait(e, *toks)

    def nop_sig(eng):
        it = R.q[eng][-1] if R.q[eng] else None
        if it is None or it[0] != 'op' or it[3] is not None:
            mset(eng, scratch1, 0.0) if eng != 'pe' else None
        return R.sig(eng)

    C0 = 0
    ident_bf = reg(C0 + 0, [128], BF16)
    ones_bf = reg(C0 + 256, [128], BF16)
    permT_bf = reg(C0 + 512, [128], BF16)
    selE_bf = reg(C0 + 768, [128], BF16)
    selO_bf = reg(C0 + 1024, [128], BF16)
    ident_f = reg(C0 + 1280, [128])
    biasv = reg(C0 + 1792, [8])
    ES = reg(C0 + 1824, [32])
    invf_t = reg(C0 + 1952, [1])
    sgn_t = reg(C0 + 1956, [1])
    scratch1 = reg(C0 + 1960, [2])
    eps_t = reg(C0 + 1968, [1])
    bglu_t = reg(C0 + 1984, [32])
    mcoef = reg(C0 + 2112, [24])
    AM = 2560
    am = reg(AM, [16, TM], BF16)
    P0 = AM + 34816

    ctmp = reg(P0, [128])
    t_c = []
    t_c.append(dma('sp', ident_f, identf_d, 'c0'))
    t_c.append(dma('sp', biasv, biasv_d, 'c0'))
    t_c.append(dma('sp', invf_t, invf2, 'c0'))
    t_c.append(dma('sp', sgn_t, sgn, 'c0'))
    t_c.append(dma('sp', mcoef, mcoef_d, 'c0'))
    t_c.append(dma('sp', ES, sink_d.partition_broadcast(128), 'c0'))
    t_c.append(dma('sp', bglu_t, b_glu.rearrange("(b p) o -> p (b o)", p=128), 'c0', nonc=True))
    R.wait('dve', t_c[-1]); R.wait('act', t_c[-1]); R.wait('pool', t_c[-1])
    cp('dve', ident_bf, ident_f)
    mset('dve', ones_bf, 1.0)
    mset('dve', eps_t, 1e-5)
    act(ES, ES, AF.Exp)
    tk = dma('sp', ctmp, permT_d, 'c1'); R.wait('dve', tk); cp('dve', permT_bf, ctmp); d1 = R.sig('dve')
    R.wait('sp', d1); tk = dma('sp', ctmp, selE_d, 'c1'); R.wait('dve', tk); cp('dve', selE_bf, ctmp); d1 = R.sig('dve')
    R.wait('sp', d1); tk = dma('sp', ctmp, selO_d, 'c1'); R.wait('dve', tk); cp('dve', selO_bf, ctmp); d1 = R.sig('dve')
    barrier()

    def sincos(eng, src, shift, tmpA, tmpB, tmpI, outs):
        for (o, sh) in outs:
            ts(eng, tmpA, src, float(shift + sh), None, ALU.add)
            ts(eng, tmpB, tmpA, 1.0 / TWO_PI, 0.5, ALU.mult, ALU.add)
            cp(eng, tmpI, tmpB)
            cp(eng, tmpB, tmpI)
            stt(tmpA, tmpB, -TWO_PI, tmpA, ALU.mult, ALU.add)
            ts(eng, tmpB, tmpA, PI, None, ALU.is_gt)
            stt(tmpA, tmpB, -TWO_PI, tmpA, ALU.mult, ALU.add)
            ts(eng, tmpB, tmpA, -PI, None, ALU.is_lt)
            stt(tmpA, tmpB, TWO_PI, tmpA, ALU.mult, ALU.add)
            t = R.sig('dve')
            R.wait('act', t)
            act(o, tmpA, AF.Sin)
            t = R.sig('act')
            R.wait('dve', t)

    out_toks = []
    for _once in (0,):
        o = [P0]
        TAoff = o[0]; o[0] += 65536
        TAz = reg(TAoff, [8, 16, 2, 64])
        TAr = reg(TAoff, [2, 64, 8, 16])
        TAflat = reg(TAoff, [16384])

        def sm(shape, dt=F32):
            r = reg(o[0], shape, dt)
            o[0] += ((int(np.prod(shape)) * (4 if dt in (F32, I32) else 2) + 3) // 4) * 4
            return r

        lre = sm([64]); lim = sm([64]); ldt = sm([1]); dtt = sm([1])
        lr = sm([64]); xx = sm([64]); th = sm([64]); mag = sm([64]); sn = sm([64]); cs = sm([64])
        den = sm([64]); rden = sm([64]); nr = sm([64]); fre = sm([64]); fim = sm([64]); e2 = sm([64])
        t1 = sm([64]); t2 = sm([64]); t3 = sm([64]); t4 = sm([64]); tI = sm([64], I32)
        PWre = sm([9, 64]); PWim = sm([9, 64]); NWre = sm([8, 64]); NWim = sm([8, 64])
        Gre = sm([8, 64]); Gim = sm([8, 64]); GNre = sm([8, 64]); GNim = sm([8, 64])
        Bre = sm([1024]); Bim = sm([1024]); Cre = sm([1024]); Cim = sm([1024])
        tbig = sm([1024])
        dskt = sm([16]); Drep = sm([128]); cmask = sm([128]); dmaskt = sm([128])
        dup = sm([2, 64]); coefT = sm([18, 128])
        sqr = sm([64]); sqi = sm([64]); p1r = sm([64]); p1i = sm([64]); p2r = sm([64]); p2i = sm([64])
        assert o[0] <= ARENA_BYTES
        Rb = reg(TAoff, [16, 128]); WTb = reg(TAoff + 8192, [16, 128])
        T2s = reg(TAoff + 16384, [16, 128]); DDt = reg(TAoff + 24576, [16, 128])

        ld = []
        for (tl, src) in ((lre, lre_d), (lim, lim_d), (ldt, ldt_d), (Bre, bre_d), (Bim, bim_d), (Cre, cre_d),
                          (Cim, cim_d), (dskt, dsk_d), (cmask, cmask_d), (dmaskt, dmask_d)):
            ld.append(dma('sp', tl, src, 'p0ld'))
        for s_ in range(8):
            ld.append(dma('sp', Drep[16 * s_:16 * s_ + 16, :], dsk_d.rearrange("g j -> j g"), 'p0ld', nonc=True))
        R.wait('dve', ld[-1]); R.wait('act', ld[-1])

        def cmul(outr, outi, ar, ai, br, bi):
            tt('dve', t1, ar, br, ALU.mult); tt('dve', t2, ai, bi, ALU.mult); tt('dve', outr, t1, t2, ALU.subtract)
            tt('dve', t1, ar, bi, ALU.mult); tt('dve', t2, ai, br, ALU.mult); tt('dve', outi, t1, t2, ALU.add)

        act(dtt, ldt, AF.Exp); a0 = R.sig('act'); R.wait('dve', a0)
        ts('dve', lr, lre, -1e-4, None, ALU.min)
        ts('dve', xx, lr, dtt[:, 0:1], None, ALU.mult)
        ts('dve', th, lim, dtt[:, 0:1], None, ALU.mult)
        d0 = R.sig('dve'); R.wait('act', d0)
        act(mag, xx, AF.Exp)
        sincos('dve', th, 8 * PI, t3, t4, tI, [(sn, 0.0), (cs, PI / 2)])
        mset('dve', PWre[:, 0, :], 1.0); mset('dve', PWim[:, 0, :], 0.0)
        tt('dve', PWre[:, 1, :], mag, cs, ALU.mult); tt('dve', PWim[:, 1, :], mag, sn, ALU.mult)
        tt('dve', t1, lr, lr, ALU.mult); tt('dve', t2, lim, lim, ALU.mult); tt('dve', den, t1, t2, ALU.add)
        R.op('dve', lambda e: e.reciprocal(out=rden, in_=den), r=[den], w=[rden])
        ts('dve', nr, PWre[:, 1, :], -1.0, None, ALU.add)
        tt('dve', t1, nr, lr, ALU.mult); tt('dve', t2, PWim[:, 1, :], lim, ALU.mult); tt('dve', t1, t1, t2, ALU.add)
        tt('dve', fre, t1, rden, ALU.mult)
        tt('dve', t1, PWim[:, 1, :], lr, ALU.mult); tt('dve', t2, nr, lim, ALU.mult); tt('dve', t1, t1, t2, ALU.subtract)
        tt('dve', fim, t1, rden, ALU.mult)
        for m in range(2, 9):
            cmul(PWre[:, m, :], PWim[:, m, :], PWre[:, m - 1, :], PWim[:, m - 1, :], PWre[:, 1, :], PWim[:, 1, :])
        for m in range(1, 9):
            act(e2, xx, AF.Exp, scale=float(-2 * m)); a0 = R.sig('act'); R.wait('dve', a0)
            tt('dve', NWre[:, m - 1, :], PWre[:, m, :], e2, ALU.mult)
            stt(NWim[:, m - 1, :], PWim[:, m, :], -1.0, e2, ALU.mult, ALU.mult)
            d0 = R.sig('dve'); R.wait('act', d0)
        for s_ in range(8):
            cmul(Gre[:, s_, :], Gim[:, s_, :], PWre[:, 7 - s_, :], PWim[:, 7 - s_, :], fre, fim)
            cmul(GNre[:, s_, :], GNim[:, s_, :], NWre[:, s_, :], NWim[:, s_, :], fre, fim)

        tb3 = tbig.rearrange("p (a b) -> p a b", a=16)
        tb3r = tbig.rearrange("p (a b) -> p a b", a=64)
        BreT = Bre.rearrange("g (p j) -> g j p", j=16); BimT = Bim.rearrange("g (p j) -> g j p", j=16)
        BreN = Bre.rearrange("g (p j) -> g p j", j=16); BimN = Bim.rearrange("g (p j) -> g p j", j=16)
        CreT = Cre.rearrange("g (c p) -> g p c", p=64); CimT = Cim.rearrange("g (c p) -> g p c", p=64)

        def bj(a):
            return a.unsqueeze(1).broadcast_to([128, 16, 64])

        def bl(a):
            return a.unsqueeze(2).broadcast_to([128, 64, 16])

        tab_free = [None]

        def flush_table(dst):
            d0 = R.sig('dve'); R.wait('sp', d0)
            tk = dma('sp', dst.rearrange("g a b -> g (a b)"), TAflat, 'p0tab')
            tab_free[0] = tk
            return tk

        def table_begin():
            R.wait('dve', tab_free[0])

        for variant in range(2):
            table_begin()
            for s_ in range(8):
                gr, gi = bj(Gre[:, s_, :]), bj(Gim[:, s_, :])
                if variant == 0:
                    tt('dve', TAz[:, s_, :, 0, :], gr, BreT, ALU.mult); tt('dve', tb3, gi, BimT, ALU.mult)
                    tt('dve', TAz[:, s_, :, 0, :], TAz[:, s_, :, 0, :], tb3, ALU.subtract)
                    tt('dve', TAz[:, s_, :, 1, :], gr, BimT, ALU.mult); tt('dve', tb3, gi, BreT, ALU.mult)
                    tt('dve', TAz[:, s_, :, 1, :], TAz[:, s_, :, 1, :], tb3, ALU.add)
                else:
                    tt('dve', TAz[:, s_, :, 1, :], gr, BreT, ALU.mult); tt('dve', tb3, gi, BimT, ALU.mult)
                    tt('dve', TAz[:, s_, :, 1, :], TAz[:, s_, :, 1, :], tb3, ALU.subtract)
                    tt('dve', TAz[:, s_, :, 0, :], gr, BimT, ALU.mult); tt('dve', tb3, gi, BreT, ALU.mult)
                    stt(TAz[:, s_, :, 0, :], TAz[:, s_, :, 0, :], -1.0, tb3, ALU.mult, ALU.subtract)
            tkWZ = flush_table(WZ_d if variant == 0 else WZP_d)
        table_begin()
        for s_ in range(8):
            gr, gi = bl(GNre[:, s_, :]), bl(GNim[:, s_, :])
            tt('dve', TAr[:, 0, :, s_, :], gr, BreN, ALU.mult); tt('dve', tb3r, gi, BimN, ALU.mult)
            tt('dve', TAr[:, 0, :, s_, :], TAr[:, 0, :, s_, :], tb3r, ALU.subtract)
            tt('dve', TAr[:, 1, :, s_, :], gr, BimN, ALU.mult); tt('dve', tb3r, gi, BreN, ALU.mult)
            tt('dve', TAr[:, 1, :, s_, :], TAr[:, 1, :, s_, :], tb3r, ALU.add)
        tkR = flush_table(R_d)
        table_begin()
        for i_ in range(8):
            pr, pi_ = bl(PWre[:, i_ + 1, :]), bl(PWim[:, i_ + 1, :])
            tt('dve', TAr[:, 0, :, i_, :], pr, CreT, ALU.mult); tt('dve', tb3r, pi_, CimT, ALU.mult)
            tt('dve', TAr[:, 0, :, i_, :], TAr[:, 0, :, i_, :], tb3r, ALU.subtract)
            tt('dve', TAr[:, 1, :, i_, :], pi_, CreT, ALU.mult); tt('dve', tb3r, pr, CimT, ALU.mult)
            stt(TAr[:, 1, :, i_, :], TAr[:, 1, :, i_, :], -1.0, tb3r, ALU.mult, ALU.subtract)
        tkW1 = flush_table(WT1_d)

        pe_free = None
        cbank = [None, None]
        tcount = [0]

        def put_coef(slot, srct):
            k_ = tcount[0]; tcount[0] += 1
            R.wait('dve', pe_free_box[0])
            cp('dve', dup[:, 0, :], srct); cp('dve', dup[:, 1, :], srct)
            d0 = R.sig('dve'); R.wait('pe', d0); R.wait('pe', cbank[k_ % 2])
            tr(psb[k_ % 2][:, 0:128], dup.rearrange("p a b -> p (a b)"), ident_f)
            pe_free_box[0] = R.sig('pe'); R.wait('act', pe_free_box[0])
            cp('act', coefT[:, slot, :], psb[k_ % 2][:, 0:128])
            cbank[k_ % 2] = R.sig('act')

        pe_free_box = [None]
        put_coef(0, PWre[:, 8, :]); put_coef(1, PWim[:, 8, :])
        cmul(sqr, sqi, PWre[:, 8, :], PWim[:, 8, :], PWre[:, 8, :], PWim[:, 8, :])
        cur = (sqr, sqi); oth = (p1r, p1i)
        put_coef(6, cur[0]); put_coef(7, cur[1])
        for j_ in range(6):
            R.wait('dve', pe_free_box[0])
            cmul(oth[0], oth[1], cur[0], cur[1], cur[0], cur[1]); cur, oth = oth, cur
            if j_ < 5:
                put_coef(8 + 2 * j_, cur[0]); put_coef(9 + 2 * j_, cur[1])
        P1 = cur
        put_coef(2, P1[0]); put_coef(3, P1[1])
        R.wait('dve', pe_free_box[0])
        cmul(p2r, p2i, P1[0], P1[1], P1[0], P1[1])
        put_coef(4, p2r); put_coef(5, p2i)
        a0 = R.sig('act'); R.wait('sp', a0)
        tk_coef = dma('sp', coef_d, coefT, 'p0c')

        R.wait('sp', tkR, tkW1)
        t2_store = None
        for gb in range(8):
            R.wait('sp', t2_store)
            if gb == 0:
                R.wait('sp', R.sig('dve'))
            l1 = dma('sp', Rb, R_d[gb * 16:(gb + 1) * 16].rearrange("g cp sj -> cp g sj"), 'p0t2')
            l2 = dma('sp', WTb, WT1_d[gb * 16:(gb + 1) * 16].rearrange("g cp ich -> cp g ich"), 'p0t2')
            R.wait('pe', l2)
            for k4 in range(4):
                bank = psb[2 + (k4 % 2)]
                if gb + k4 > 0:
                    R.wait('pe', R.sig('dve'))
                for g_ in range(4):
                    gg = k4 * 4 + g_
                    mm(bank[:, g_ * 128:(g_ + 1) * 128], Rb[:, gg, :], WTb[:, gg, :])
                pt = R.sig('pe'); R.wait('dve', pt)
                tt('dve', T2s[:, k4 * 4:(k4 + 1) * 4, :], bank[:, :].rearrange("p (a b) -> p a b", a=4),
                   cmask.unsqueeze(1).broadcast_to([128, 4, 128]), ALU.mult)
            tt('dve', DDt, dmaskt.unsqueeze(1).broadcast_to([128, 16, 128]),
               Drep[:, gb * 16:(gb + 1) * 16].unsqueeze(2).broadcast_to([128, 16, 128]), ALU.mult)
            tt('dve', T2s, T2s, DDt, ALU.add)
            d0 = R.sig('dve'); R.wait('sp', d0)
            t2_store = dma('sp', T2_d[:, gb * 16:(gb + 1) * 16, :], T2s, 'p0t2')
        tk_tables = [tkWZ, t2_store, tk_coef]
        barrier()
        if stop == 'A':
            break
        HT = P0
        hT = reg(HT, [32, TM], BF16)
        HH = HT + 69632
        hTh = reg(HH, [32, 128], BF16)
        QT = reg(HH, [4, TM], BF16)
        WB = HH + 8704
        wbuf = [wb_t[0][:, :, :], wb_t[1][:, :, :]]
        KTo = WB
        KT2 = reg(KTo, [4, 1280], BF16)
        KcT = reg(KTo + 10240, [4, 128], BF16)
        VDo = KTo + 11264
        Vd = reg(VDo, [10, 4, 2, 64], BF16)
        Vcd = reg(VDo + 10240, [4, 2, 64], BF16)
        RTo = VDo + 11264
        cosT = reg(RTo, [1216]); sinS = reg(RTo + 4864, [1216])
        TMP = RTo + 9728
        xt = [reg(WB, [D]), reg(WB + 16384, [D])]
        xn = reg(WB + 32768, [D], BF16)
        junk = reg(WB + 40960, [2048], BF16)
        g_rep = reg(WB + 45056, [D])
        ssq = reg(WB + 61440, [32]); rstd = reg(WB + 61568, [16])

        tg = dma('sp', g_rep, norm_g.partition_broadcast(128), 'xa')
        R.wait('dve', tg)
        xfree = [None, None]
        xnfree = None
        bankfree = [None] * 8
        for t in range(10):
            rows = 64 if t == 9 else 128
            r0 = t * 128
            sl = t % 2
            R.wait('sp', xfree[sl])
            tk = dma('sp', xt[sl][0:rows, :], xs[r0:r0 + rows, :], f'x{sl}')
            R.wait('act', tk)
            act(junk[0:rows, :], xt[sl][0:rows, 0:2048], AF.Square, accum=ssq[0:rows, 2 * t:2 * t + 1])
            act(junk[0:rows, :], xt[sl][0:rows, 2048:4096], AF.Square, accum=ssq[0:rows, 2 * t + 1:2 * t + 2])
            a0 = R.sig('act'); R.wait('dve', a0)
            tt('dve', ssq[0:rows, 2 * t:2 * t + 1], ssq[0:rows, 2 * t:2 * t + 1], ssq[0:rows, 2 * t + 1:2 * t + 2], ALU.add)
            ts('dve', rstd[0:rows, t:t + 1], ssq[0:rows, 2 * t:2 * t + 1], 1.0 / D, 1e-5, ALU.mult, ALU.add)
            d0 = R.sig('dve'); R.wait('act', d0)
            act(rstd[0:rows, t:t + 1], rstd[0:rows, t:t + 1], AF.Sqrt)
            a0 = R.sig('act'); R.wait('dve', a0)
            R.op('dve', lambda e, o_=rstd[0:rows, t:t + 1]: e.reciprocal(out=o_, in_=o_), r=[rstd[0:rows, t:t + 1]], w=[rstd[0:rows, t:t + 1]])
            R.wait('dve', xnfree)
            stt(xn[0:rows, :], xt[sl][0:rows, :], rstd[0:rows, t:t + 1], g_rep[0:rows, :], ALU.mult, ALU.mult)
            d0 = R.sig('dve'); xfree[sl] = d0
            R.wait('pe', d0)
            for b4 in range(4):
                bk = (t % 2) * 4 + b4
                R.wait('pe', bankfree[bk])
                pbf = psb[bk][:, :].bitcast(BF16)
                for j in range(8):
                    kt = b4 * 8 + j
                    tr(pbf[0:128, j * 128:j * 128 + rows], xn[0:rows, kt * 128:(kt + 1) * 128], ident_bf[0:rows, 0:rows])
                p0 = R.sig('pe')
                if b4 == 3:
                    xnfree = p0
                eng = 'act' if b4 % 2 == 0 else 'dve'
                R.wait(eng, p0)
                src = pbf[:, :].rearrange("p (a b) -> p a b", a=8)[:, :, 0:rows]
                if t == 0:
                    dst = hTh[:, b4 * 8:(b4 + 1) * 8, 0:128]
                else:
                    dst = hT[:, b4 * 8:(b4 + 1) * 8, (t - 1) * 128:(t - 1) * 128 + rows]
                cp(eng, dst, src)
                bankfree[bk] = R.sig(eng)
        barrier()
        if stop == 'B':
            break
        qraw = [reg(TMP, [448], BF16), reg(TMP + 896, [448], BF16)]
        Aq = [reg(TMP + 1792, [448]), reg(TMP + 3584, [448])]
        Bq = [reg(TMP + 5376, [448]), reg(TMP + 7168, [448])]
        PTA = [reg(TMP + 8960, [512], BF16), reg(TMP + 9984, [512], BF16)]
        PTB = [reg(TMP + 11008, [512], BF16), reg(TMP + 12032, [512], BF16)]
        dtot = reg(TMP + 13056, [512]); tnm = reg(TMP + 15104, [512])
        stg = [reg(TMP + 17152, [TM], BF16), reg(TMP + 19328, [TM], BF16)]
        kvo = [reg(TMP + 21504, [256]), reg(TMP + 22528, [256])]
        krotf = reg(TMP + 23552, [448])
        krbf = reg(TMP + 25344, [448], BF16)
        onesL = reg(TMP + 26240, [128], BF16); onesR = reg(TMP + 26496, [128], BF16)
        ES2 = reg(TMP + 26752, [16])
        den2 = reg(TMP + 26816, [256]); tbuf = [reg(TMP + 27840, [512]), reg(TMP + 29888, [512])]
        assert TMP + 31936 <= ARENA_BYTES
        ang = reg(TMP, [1216]); rtA = reg(TMP + 4864, [1216]); rtB = reg(TMP + 9728, [1216]); rtI = reg(TMP + 14592, [1216], I32)
        assert TMP + 19456 <= ARENA_BYTES
        tk = dma('sp', ang, posv.partition_broadcast(128), 'c1')
        R.wait('dve', tk)
        ts('dve', ang, ang, invf_t[:, 0:1], None, ALU.mult)
        sincos('dve', ang, 0.0, rtA, rtB, rtI, [(sinS, 0.0), (cosT, PI / 2)])
        ts('dve', sinS, sinS, sgn_t[:, 0:1], None, ALU.mult)
        mset('dve', KT2[:, :, 1216:1280], 0.0)
        mset('dve', onesL[:, 0:64], 1.0); mset('dve', onesL[:, 64:128], 0.0)
        mset('dve', onesR[:, 0:64], 0.0); mset('dve', onesR[:, 64:128], 1.0)
        ESv = ES.rearrange("p (g h) -> p g h", h=2)
        cp('dve', ES2[0:64, :], ESv[0:64, :, 0]); cp('dve', ES2[64:128, :], ESv[64:128, :, 1])
        mset('dve', Vd[:, 9, :, :, :], 0.0)
        barrier()

        if stop == 'Brope':
            break
        w_in_v = w_in.rearrange("(kt p) c -> p kt c", p=128)

        class WS:
            def __init__(self):
                self.i = 0
                self.rel = {}

            def load(self, srcs, bufs=None):
                b = wbuf if bufs is None else bufs
                sl = self.i % 2
                R.wait('pool', self.rel.get(self.i - 2))
                tok = None
                for (k0, nk, src) in srcs:
                    tok = dma('pool', b[sl][:, k0:k0 + nk, :], src, f'w{sl}')
                self.i += 1
                return b[sl], tok, self.i - 1

            def release(self, idx, tok):
                self.rel[idx] = tok

        ws = WS()
        pcount = [0]

        def proj_fm(wb, wtok, nmblk, rhs_list, epilogue, col0=0, nkt=32, kt0=0):
            R.wait('pe', wtok)
            last = None
            for mb in range(nmblk):
                base = (pcount[0] % 2) * 3 if len(rhs_list) <= 3 else 0
                pcount[0] += 1
                banks = [psb[base + p] for p in range(len(rhs_list))]
                for p in range(len(rhs_list)):
                    R.wait('pe', bankfree[base + p])
                for kt in range(nkt):
                    for p, rf in enumerate(rhs_list):
                        r_ap, n = rf(kt)
                        mm(banks[p][:, 0:n], wb[:, kt0 + kt, col0 + mb * 128:col0 + (mb + 1) * 128], r_ap,
                           start=(kt == 0), stop=(kt == nkt - 1))
                last = R.sig('pe')
                frees = epilogue(mb, banks, last)
                for p in range(len(rhs_list)):
                    bankfree[base + p] = frees[p]
            return last

        def rhs_main(a, n):
            return lambda kt: (hT[:, kt, a:a + n], n)

        main_rhs = [rhs_main(a, n) for (a, n) in PIECES]
        stgfree = [None, None]
        stgcnt = [0]

        wb, wtok, widx = ws.load([(0, 32, w_in_v[:, :, C_K:C_K + 256])])
        ckf = reg(TMP + 13056, [256])
        ckb = reg(TMP, [4, 2, 64], BF16)
        tk = dma('sp', ckf, ck_d, 'c1'); R.wait('dve', tk)
        ckf3 = ckf.rearrange("p (g d) -> p g d", g=4)
        cp('dve', ckb[:, :, 0, :], ckf3); cp('dve', ckb[:, :, 1, :], ckf3)
        d0 = R.sig('dve'); R.wait('pe', d0)
        pbf7 = psb[7][:, :].bitcast(BF16)
        for G in range(4):
            tr(pbf7[:, G * 128:(G + 1) * 128], ckb[:, G, :, :].rearrange("p a b -> p (a b)"), ident_bf)
        p0 = R.sig('pe'); R.wait('act', p0)
        cp('act', KcT, pbf7[:, 0:512].rearrange("p (a b) -> p a b", a=4))
        bankfree[7] = R.sig('act')
        R.wait('sp', d0)
        tk = dma('sp', ckf, cv_d, 'c1'); R.wait('dve', tk)
        cp('dve', Vcd[:, :, 0, :], ckf3); cp('dve', Vcd[:, :, 1, :], ckf3)
        kv_tmp_free = R.sig('dve')

        if stop == 'Bck':
            break
        slotc = [0]
        qraw_free = [None, None]

        def rope_piece(bank, n, tau0, out_bf, out_f32=None):
            sl = slotc[0] % 2
            slotc[0] += 1
            R.wait('act', qraw_free[sl])
            cp('act', qraw[sl][:, 0:n], bank[:, 0:n]); a0 = R.sig('act')
            R.wait('dve', a0)
            tt('dve', Aq[sl][:, 0:n], bank[:, 0:n], cosT[:, tau0:tau0 + n], ALU.mult); d0 = R.sig('dve')
            R.wait('pe', a0); R.wait('pe', bankfree[6 + sl])
            mm(psb[6 + sl][:, 0:n], permT_bf, qraw[sl][:, 0:n])
            p0 = R.sig('pe'); R.wait('dve', p0)
            qraw_free[sl] = p0
            tt('dve', Bq[sl][:, 0:n], psb[6 + sl][:, 0:n], sinS[:, tau0:tau0 + n], ALU.mult)
            bankfree[6 + sl] = R.sig('dve')
            if out_f32 is not None:
                tt('dve', out_f32, Aq[sl][:, 0:n], Bq[sl][:, 0:n], ALU.add)
                cp('dve', out_bf, out_f32)
            else:
                tt('dve', out_bf, Aq[sl][:, 0:n], Bq[sl][:, 0:n], ALU.add)
            return [a0, d0]

        R.wait('dve', kv_tmp_free); R.wait('act', kv_tmp_free)
        k_rhs = [lambda kt: (hTh[:, kt, 0:128], 128)] + main_rhs
        k_tau = [(0, 128)] + [(128 + a, n) for (a, n) in PIECES]

        def epi_k(mb, banks, ptok):
            frees = []
            R.wait('act', ptok); R.wait('dve', ptok)
            for p, (tau0, n) in enumerate(k_tau):
                keepf = (p == 3)
                fr = rope_piece(banks[p], n, tau0, krbf[:, 0:n], krotf[:, 0:n] if keepf else None)
                frees.append(fr)
                d0 = R.sig('dve'); R.wait('pe', d0)
                if keepf:
                    R.wait('pe', bankfree[4])
                    tr(psb[4][:, 0:128], krotf[:, 128:256], ident_f)
                    tr(psb[4][:, 128:256], krotf[:, 192:320], ident_f)
                    p1 = R.sig('pe'); R.wait('act', p1)
                    cp('act', kvo[0][:, mb * 128:(mb + 1) * 128], psb[4][:, 0:128])
                    cp('act', kvo[1][:, mb * 128:(mb + 1) * 128], psb[4][:, 128:256])
                    bankfree[4] = R.sig('act')
                for h, sel in enumerate((selE_bf, selO_bf)):
                    G = mb * 2 + h
                    sl = slotc[0] % 2; slotc[0] += 1
                    R.wait('pe', bankfree[6 + sl])
                    mm(psb[6 + sl][:, 0:n], sel, krbf[:, 0:n])
                    p0 = R.sig('pe'); R.wait('act', p0)
                    cp('act', KT2[:, G, tau0:tau0 + n], psb[6 + sl][:, 0:n])
                    bankfree[6 + sl] = R.sig('act')
                R.wait('dve', sig_last('pe'))
                R.wait('act', sig_last('pe'))
            return frees

        lastk = proj_fm(wb, wtok, 2, k_rhs, epi_k)
        ws.release(widx, lastk)
        a0 = sig_last('act'); R.wait('sp', a0)
        out_toks.append(dma('sp', kp_o, kvo[0], 'out'))
        out_toks.append(dma('sp', ks_o, kvo[1][64:128, :], 'out'))
        kvo_free = out_toks[-1]

        if stop == 'Bk':
            break
        wb, wtok, widx = ws.load([(0, 32, w_in_v[:, :, C_V:C_V + 256])])
        R.wait('pe', wtok)
        vst = [reg(TMP + 13056, [256]), reg(TMP + 14080, [256])]
        for t in range(10):
            rows = 64 if t == 9 else 128
            bk = t % 2
            R.wait('pe', bankfree[bk])
            for kt in range(32):
                lt = hTh[:, kt, 0:128] if t == 0 else hT[:, kt, (t - 1) * 128:(t - 1) * 128 + rows]
                mm(psb[bk][0:rows, 0:256], lt, wb[:, kt, :], start=(kt == 0), stop=(kt == 31))
            p0 = R.sig('pe'); R.wait('act', p0)
            pv = psb[bk][0:rows, 0:256].rearrange("p (g d) -> p g d", g=4)
            cp('act', Vd[0:rows, t, :, 0, :], pv); cp('act', Vd[0:rows, t, :, 1, :], pv)
            if t >= 8:
                if t == 8:
                    R.wait('act', kvo_free)
                cp('act', vst[t - 8][0:rows, :], psb[bk][0:rows, 0:256])
                a0 = R.sig('act'); R.wait('sp', a0)
                out_toks.append(dma('sp', vp_o if t == 8 else vs_o, vst[t - 8][0:rows, :], 'out'))
            bankfree[bk] = R.sig('act')
        ws.release(widx, sig_last('pe'))
        v_out_done = out_toks[-1]

        if stop == 'Bkv':
            break
        def epi_za(blk):
            def f(mb, banks, ptok):
                R.wait('act', ptok)
                frees = []
                for p, (a, n) in enumerate(PIECES):
                    act(am[:, blk * 2 + mb, a:a + n], banks[p][:, 0:n], AF.Silu)
                    frees.append(R.sig('act'))
                return frees
            return f

        for c in range(8):
            wb, wtok, widx = ws.load([(0, 32, w_in_v[:, :, C_ZA + c * 256:C_ZA + (c + 1) * 256])])
            lt = proj_fm(wb, wtok, 2, main_rhs, epi_za(c))
            ws.release(widx, lt)

        if stop == 'Bza':
            break
        R.wait('dve', v_out_done); R.wait('act', v_out_done)
        qt_free = [None]

        def epi_q(cq):
            def f(mb, banks, ptok):
                R.wait('act', ptok); R.wait('dve', ptok)
                frees = []
                for p, (a, n) in enumerate(PIECES):
                    frees.append(rope_piece(banks[p], n, 128 + a, QT[:, cq * 2 + mb, a:a + n]))
                return frees
            return f

        pt_free = [None, None]
        tb_free = [None, None]
        for G in range(4):
            R.wait('dve', qt_free[0])
            for cq in range(2):
                cidx = G * 2 + cq
                wb, wtok, widx = ws.load([(0, 32, w_in_v[:, :, C_Q + cidx * 256:C_Q + (cidx + 1) * 256])])
                lt = proj_fm(wb, wtok, 2, main_rhs, epi_q(cq))
                ws.release(widx, lt)
            qt_ready = R.sig('dve')
            R.wait('pe', qt_ready)
            for cc in range(17):
                nb = 4 + (cc % 2) * 2
                SAe, SAo, SBe, SBo = psb[0], psb[1], psb[2], psb[3]
                NUMb, DENb = psb[nb], psb[nb + 1]
                qa = 64 * cc
                if cc < 16:
                    tA = (64 * cc) // 128
                    ktA = KT2[:, G, tA * 128:(tA + 1) * 128]; ktB = KT2[:, G, (tA + 1) * 128:(tA + 2) * 128]
                    vA = Vd[:, tA, G, :, :]; vB = Vd[:, tA + 1, G, :, :]
                    if cc == 0:
                        bA, bB = 3, 2
                    elif cc == 1:
                        bA, bB = 4, 0
                    elif cc % 2 == 0:
                        bA, bB = 0, 2
                    else:
                        bA, bB = 1, 0
                else:
                    ktA = KcT[:, G, :]; ktB = KT2[:, G, 1152:1280]
                    vA = Vcd[:, G, :, :]; vB = Vd[:, 9, G, :, :]
                    bA, bB = 0, 2
                for b_ in (0, 1, 2, 3, nb, nb + 1):
                    R.wait('pe', bankfree[b_])
                for (Se, So, ktile) in ((SAe, SAo, ktA), (SBe, SBo, ktB)):
                    for pl in range(4):
                        mm(Se[:, pl * 64:(pl + 1) * 64], ktile[0:64, :], QT[0:64, pl, qa:qa + 64])
                        mm(So[:, pl * 64:(pl + 1) * 64], ktile[64:128, :], QT[64:128, pl, qa:qa + 64])
                p0 = R.sig('pe'); R.wait('act', p0)
                sl = cc % 2
                R.wait('act', pt_free[sl])
                act(PTA[sl][:, 0:256], SAe[:, 0:256], AF.Exp, bias=biasv[:, bA:bA + 1], scale=0.125)
                act(PTA[sl][:, 256:512], SAo[:, 0:256], AF.Exp, bias=biasv[:, bA:bA + 1], scale=0.125)
                act(PTB[sl][:, 0:256], SBe[:, 0:256], AF.Exp, bias=biasv[:, bB:bB + 1], scale=0.125)
                act(PTB[sl][:, 256:512], SBo[:, 0:256], AF.Exp, bias=biasv[:, bB:bB + 1], scale=0.125)
                a0 = R.sig('act'); R.wait('pe', a0)
                for b_ in range(4):
                    bankfree[b_] = a0
                mm(NUMb[:, :], vA.rearrange("p a b -> p (a b)"), PTA[sl], start=True, stop=False)
                mm(NUMb[:, :], vB.rearrange("p a b -> p (a b)"), PTB[sl], start=False, stop=True)
                mm(DENb[:, 0:256], onesL, PTA[sl][:, 0:256], start=True, stop=False)
                mm(DENb[:, 0:256], onesR, PTA[sl][:, 256:512], start=False, stop=False)
                mm(DENb[:, 0:256], onesL, PTB[sl][:, 0:256], start=False, stop=False)
                mm(DENb[:, 0:256], onesR, PTB[sl][:, 256:512], start=False, stop=True)
                p1 = R.sig('pe'); pt_free[sl] = p1
                R.wait('dve', p1)
                tt('dve', den2.rearrange("p (a t) -> p a t", a=4), DENb[:, 0:256].rearrange("p (a t) -> p a t", a=4),
                   ES2[:, G * 4:(G + 1) * 4].unsqueeze(2).broadcast_to([128, 4, 64]), ALU.add)
                R.op('dve', lambda e: e.reciprocal(out=den2, in_=den2), r=[den2], w=[den2])
                tb_ = tbuf[cc % 2]
                tt('dve', tb_[0:64, 0:256], NUMb[0:64, 0:256], den2[0:64, :], ALU.mult)
                tt('dve', tb_[64:128, 0:256], NUMb[64:128, 256:512], den2[64:128, :], ALU.mult)
                d1 = R.sig('dve'); bankfree[nb] = d1; bankfree[nb + 1] = d1
                dst = am[:, G * 4:(G + 1) * 4, qa:qa + 64]
                tt('dve', dst, tb_[:, 0:256].rearrange("p (a b) -> p a b", a=4), dst, ALU.mult)
            qt_free[0] = sig_last('pe')
        barrier()
        if stop == 'B2':
            break
        U_rows = U_dram.rearrange("g j s n -> (g j) (s n)")

        def epi_spill(kind, blkidx_fn):
            def f(mb, banks, ptok):
                sl = stgcnt[0] % 2
                stgcnt[0] += 1
                R.wait('act', ptok); R.wait('act', stgfree[sl])
                frees = []
                for p, (a, n) in enumerate(PIECES):
                    if kind == 'u':
                        dst = stg[sl].rearrange("p (s n) -> p s n", s=8)[:, :, a // 8:(a + n) // 8]
                        src = banks[p][:, 0:n].rearrange("p (n s) -> p s n", s=8)
                        cp('act', dst, src)
                    else:
                        act(stg[sl][:, a:a + n], banks[p][:, 0:n], AF.Silu if kind == 'zs' else AF.Sigmoid)
                    frees.append(R.sig('act'))
                R.wait('sp', frees[-1])
                bi = blkidx_fn(mb)
                if kind == 'u':
                    dstd = U_rows[bi * 128:(bi + 1) * 128, :]
                elif kind == 'zs':
                    dstd = szs_dram[bi]
                else:
                    dstd = gate_dram[bi]
                stgfree[sl] = dma('sp', dstd, stg[sl], f'stg{sl}')
                return frees
            return f

        spill_toks = []
        for (kind, c0, nchunks, boff) in (('u', C_U, 8, 0), ('zs', C_ZS, 8, 0), ('g', C_GA, 16, 0), ('g', C_GS, 16, 32)):
            for c in range(nchunks):
                wb, wtok, widx = ws.load([(0, 32, w_in_v[:, :, c0 + c * 256:c0 + (c + 1) * 256])])
                lt = proj_fm(wb, wtok, 2, main_rhs, epi_spill(kind, lambda mb, c=c, boff=boff: boff + c * 2 + mb))
                ws.release(widx, lt)
        spill_done = [stgfree[0], stgfree[1]]
        barrier()
        if stop == 'C':
            break
        oc = P0
        U8 = reg(oc, [64, NCH], BF16); oc += 17408
        TABA = wb_t[0][:, :, :].rearrange("p a b -> p (a b)").rearrange("p (g c) -> p g c", g=64)
        TABB = wb_t[1][:, :, :].rearrange("p a b -> p (a b)").rearrange("p (g c) -> p g c", g=64)
        Zo = oc; Zt = reg(oc, [64, NCH]); Y8 = reg(oc, [64, NCH], BF16); oc += 34816
        ZPo = oc; ZPt = reg(oc, [64, NCH]); Vb = reg(oc, [64, NCH], BF16); oc += 34816
        hist = reg(oc, [64, 129]); oc += 33024
        coefC = reg(oc, [6, 128]); oc += 3072
        Gg = reg(oc, [8, 256]); oc += 8192
        Fall = reg(oc, [256]); oc += 1024
        Ht = reg(oc, [128]); Hp = reg(oc + 512, [128]); oc += 1024
        vpp = [reg(oc, [64]), reg(oc + 256, [64])]; oc += 512
        vps = [reg(oc, [64]), reg(oc + 256, [64])]; oc += 512
        w1 = reg(oc, [64]); w2 = reg(oc + 256, [64]); w3 = reg(oc + 512, [64]); w4 = reg(oc + 768, [64]); oc += 1024
        hs = reg(oc, [64, 9]); oc += 2304
        S0T = reg(oc, [128]); S0pT = reg(oc + 512, [128]); oc += 1024
        PF = reg(oc, [128]); SF = reg(oc + 512, [128]); oc += 1024
        s0n = reg(oc, [128]); oc += 512
        s0n2 = reg(oc, [128]); oc += 512
        assert oc <= ARENA_BYTES, oc
        Tk = [reg(Zo + 512 * i, [128]) for i in range(6)]

        R.wait('sp', tk_tables, spill_done)
        R.wait('pool', tk_tables, spill_done)
        tkc = dma('sp', coefC, coef_d[:, 0:6, :], 'cf')
        coefX = reg(Zo + 69632 + 33024 + 3072, [12, 128])
        tkc = dma('sp', coefX, coef_d[:, 6:18, :], 'cf')
        t_a = dma('sp', s0n[:, 0:64], s0re_d, 'c1'); t_a = dma('sp', s0n[:, 64:128], s0im_d, 'c1')
        R.wait('dve', t_a)
        cp('dve', s0n2[:, 64:128], s0n[:, 0:64])
        ts('dve', s0n2[:, 0:64], s0n[:, 64:128], -1.0, None, ALU.mult)
        d0 = R.sig('dve'); R.wait('pe', d0)
        tr(psb[6][:, 0:128], s0n, ident_f); tr(psb[6][:, 128:256], s0n2, ident_f)
        p0 = R.sig('pe'); R.wait('act', p0)
        cp('act', S0T, psb[6][:, 0:128]); cp('act', S0pT, psb[6][:, 128:256])
        s0_ready = R.sig('act')
        bankfree[6] = s0_ready

        zstate = dict(u8free=None, tabfree=None, zfree=None)

        def zphase(hf, tabs):
            g0 = 64 * hf
            R.wait('sp', zstate['u8free']); R.wait('pool', zstate['tabfree'])
            tk_u = None
            for s_ in range(8):
                tk_u = dma('sp', U8[16 * s_:16 * s_ + 16, :, :],
                           U_dram[g0:g0 + 64, :, s_, :].rearrange("g j n -> j g n"), 'zu')
            ta = dma('pool', TABA, tabs[0], 'zt')
            tb = dma('pool', TABB, tabs[1], 'zt')
            return tk_u, tb

        def zmatmuls(tk_u, tb):
            R.wait('pe', tk_u, tb); R.wait('act', zstate['zfree'])
            for (tab, dst) in ((TABA, Zt), (TABB, ZPt)):
                for g3 in range(0, 64, 3):
                    ng = min(3, 64 - g3)
                    bk = (g3 // 3) % 4
                    R.wait('pe', bankfree[bk])
                    for k in range(ng):
                        mm(psb[bk][:, k * NCH:(k + 1) * NCH], tab[:, g3 + k, :], U8[:, g3 + k, :])
                    p0 = R.sig('pe'); R.wait('act', p0)
                    cp('act', dst[:, g3:g3 + ng, :], psb[bk][:, 0:ng * NCH].rearrange("p (a b) -> p a b", a=ng))
                    bankfree[bk] = R.sig('act')
            zstate['tabfree'] = sig_last('pe')
            return sig_last('act')

        def scan_steps(hf, n0, nsteps, hbuf, vbuf, zoff, zready):
            g0 = 64 * hf
            Ar = coefC[:, 0, g0:g0 + 64]; Ai = coefC[:, 1, g0:g0 + 64]
            R.wait('dve', zready)
            R.wait('dve', sig_last('pool'))
            for i in range(nsteps):
                Vp = hbuf[:, :, i]; Vn = hbuf[:, :, i + 1]
                VPp = vbuf[i % 2]; VPn = vbuf[(i + 1) % 2]
                tt('dve', w1, Ar, Vp, ALU.mult)
                tt('dve', w3, Ar, VPp, ALU.mult)
                tt('dve', w2, Ai, VPp, ALU.mult)
                tt('dve', w4, Ai, Vp, ALU.mult)
                tt('dve', w1, w1, w2, ALU.add)
                tt('dve', w3, w3, w4, ALU.subtract)
                tt('dve', Vn, w1, Zt[:, :, zoff + i], ALU.add)
                tt('dve', VPn, w3, ZPt[:, :, zoff + i], ALU.add)

        R.wait('dve', tkc); R.wait('pool', tkc)
        tt1 = reg(ZPo + 34816, [64, 64]); tt2 = reg(ZPo + 34816 + 16384, [64, 64])
        def coef_k(k, g0):
            if k == 0:
                return coefC[:, 0, g0:g0 + 64], coefC[:, 1, g0:g0 + 64]
            return coefX[:, 2 * k - 2, g0:g0 + 64], coefX[:, 2 * k - 1, g0:g0 + 64]

        def sweep_level(k, g0, s_lo, d_lo, cnt):
            st2 = 2 << k
            Ark, Aik = coef_k(k, g0)
            Arb = Ark.unsqueeze(2).broadcast_to([128, 64, cnt]); Aib = Aik.unsqueeze(2).broadcast_to([128, 64, cnt])
            hi_s = s_lo + (cnt - 1) * st2 + 1; hi_d = d_lo + (cnt - 1) * st2 + 1
            Zs = Zt[:, :, s_lo:hi_s:st2]; Zd = Zt[:, :, d_lo:hi_d:st2]
            ZPs = ZPt[:, :, s_lo:hi_s:st2]; ZPd = ZPt[:, :, d_lo:hi_d:st2]
            ta = tt1[:, :, 0:cnt]; tb2 = tt2[:, :, 0:cnt]
            tt('dve', ta, Arb, Zs, ALU.mult)
            tt('dve', tb2, Aib, ZPs, ALU.mult)
            tt('dve', Zd, Zd, ta, ALU.add)
            tt('dve', Zd, Zd, tb2, ALU.add)
            tt('dve', ta, Arb, ZPs, ALU.mult)
            tt('dve', tb2, Aib, Zs, ALU.mult)
            tt('dve', ZPd, ZPd, ta, ALU.add)
            tt('dve', ZPd, ZPd, tb2, ALU.subtract)

        def up_sweep(g0):
            for k in range(7):
                st_ = 1 << k
                sweep_level(k, g0, st_ - 1, 2 * st_ - 1, 64 >> k)

        def down_sweep(g0):
            for k in range(5, -1, -1):
                st_ = 1 << k
                sweep_level(k, g0, 2 * st_ - 1, 3 * st_ - 1, (64 >> k) - 1)

        for hf in range(2):
            g0 = 64 * hf
            tk_u, tb = zphase(hf, (WZ_d[g0:g0 + 64].rearrange("g sj cp -> sj g cp"),
                                   WZP_d[g0:g0 + 64].rearrange("g sj cp -> sj g cp")))
            zr = zmatmuls(tk_u, tb)
            R.wait('dve', zr)
            up_sweep(g0)
            cp('dve', Fall[:, g0:g0 + 64], Zt[:, :, 127]); cp('dve', Fall[:, 128 + g0:128 + g0 + 64], ZPt[:, :, 127])
            zstate['zfree'] = [sig_last('dve'), sig_last('pool')]
            zstate['u8free'] = sig_last('pe')
        d0 = R.sig('dve'); R.wait('sp', d0)
        tf = dma('sp', cc_in, Fall, 'cc')
        R.wait('pool', tf)
        if nocc:
            tcc = None
            for r_ in range(8):
                tcc = dma('pool', cc_out[r_ * 128:(r_ + 1) * 128, :], cc_in, 'ccx')
            R.wait('pool', tcc)
        else:
            R.op('pool', lambda e: e.collective_compute("AllGather", ALU.bypass, replica_groups=[list(range(NCORES))],
                                                        ins=[cc_in], outs=[cc_out]))
            ctok = R.sig('pool')
            R.q['pool'].append(['wait', ctok])
        tg_ = dma('pool', Gg, cc_out.rearrange("(r p) c -> p r c", p=128), 'cc2')
        R.wait('dve', tg_)
        for k in range(3):
            for part in range(2):
                acc = Tk[k * 2 + part]
                ts('dve', acc, Gg[:, 0, part * 128:(part + 1) * 128], mcoef[:, k:k + 1], None, ALU.mult)
                for c2 in range(1, 8):
                    stt(acc, Gg[:, c2, part * 128:(part + 1) * 128], mcoef[:, 3 * c2 + k:3 * c2 + k + 1], acc, ALU.mult, ALU.add)
        T0, T0p, T1, T1p, T2_, T2p = Tk
        P1r, P1i, P2r, P2i = coefC[:, 2, :], coefC[:, 3, :], coefC[:, 4, :], coefC[:, 5, :]
        wb1 = reg(Zo + 3072, [128]); wb2 = reg(Zo + 3584, [128])
        tt('dve', Ht, P1r, T1, ALU.mult); tt('dve', wb1, P1i, T1p, ALU.mult); tt('dve', Ht, Ht, wb1, ALU.add)
        tt('dve', wb1, P2r, T2_, ALU.mult); tt('dve', Ht, Ht, wb1, ALU.add)
        tt('dve', wb1, P2i, T2p, ALU.mult); tt('dve', Ht, Ht, wb1, ALU.add); tt('dve', Ht, Ht, T0, ALU.add)
        tt('dve', Hp, P1r, T1p, ALU.mult); tt('dve', wb1, P1i, T1, ALU.mult); tt('dve', Hp, Hp, wb1, ALU.subtract)
        tt('dve', wb1, P2r, T2p, ALU.mult); tt('dve', Hp, Hp, wb1, ALU.add)
        tt('dve', wb1, P2i, T2_, ALU.mult); tt('dve', Hp, Hp, wb1, ALU.subtract); tt('dve', Hp, Hp, T0p, ALU.add)
        zstate['zfree'] = [sig_last('dve'), sig_last('pool')]
        R.wait('sp', sig_last('dve'))
        tkx = dma('sp', coefX, coef_d[:, 6:18, :], 'cf')
        R.wait('dve', tkx)

        y8free = None
        for hf in range(2):
            g0 = 64 * hf
            tk_u, tb = zphase(hf, (WZ_d[g0:g0 + 64].rearrange("g sj cp -> sj g cp"),
                                   WZP_d[g0:g0 + 64].rearrange("g sj cp -> sj g cp")))
            R.wait('act', y8free)
            zr = zmatmuls(tk_u, tb)
            R.wait('dve', zr); R.wait('dve', s0_ready)
            Ar0 = coefC[:, 0, g0:g0 + 64]; Ai0 = coefC[:, 1, g0:g0 + 64]
            Hh = Ht[:, g0:g0 + 64]; Hph = Hp[:, g0:g0 + 64]
            tt('dve', w1, Ar0, Hh, ALU.mult); tt('dve', w2, Ai0, Hph, ALU.mult)
            tt('dve', w3, Ar0, Hph, ALU.mult); tt('dve', w4, Ai0, Hh, ALU.mult)
            tt('dve', Zt[:, :, 0], Zt[:, :, 0], w1, ALU.add); tt('dve', ZPt[:, :, 0], ZPt[:, :, 0], w3, ALU.add)
            tt('dve', Zt[:, :, 0], Zt[:, :, 0], w2, ALU.add); tt('dve', ZPt[:, :, 0], ZPt[:, :, 0], w4, ALU.subtract)
            up_sweep(g0)
            down_sweep(g0)
            cp('dve', hs[:, :, 0], S0T[:, g0:g0 + 64]); cp('dve', vps[0], S0pT[:, g0:g0 + 64])
            scan_steps(hf, 0, 8, hs, vps, 128, zr)
            cp('dve', PF[:, g0:g0 + 64], Zt[:, :, 127]); cp('dve', SF[:, g0:g0 + 64], hs[:, :, 8])
            cp('dve', hist[:, :, 0], Hh); cp('dve', hist[:, :, 1:128], Zt[:, :, 0:127])
            cp('dve', Vb[:, :, 0:128], hist[:, :, 0:128]); cp('dve', Vb[:, :, 128:136], hs[:, :, 0:8])
            vb_ready = R.sig('dve')
            R.wait('pool', zstate['tabfree'])
            ta = dma('pool', TABA, WT1_d[g0:g0 + 64].rearrange("g cp ich -> cp g ich"), 'zt')
            tb = dma('pool', TABB, T2_d[:, g0:g0 + 64, :], 'zt')
            R.wait('pe', tb, vb_ready); R.wait('act', vb_ready)
            for g3 in range(0, 64, 3):
                ng = min(3, 64 - g3)
                bk = (g3 // 3) % 4
                R.wait('pe', bankfree[bk])
                for k in range(ng):
                    mm(psb[bk][:, k * NCH:(k + 1) * NCH], TABA[:, g3 + k, :], Vb[:, g3 + k, :], start=True, stop=False)
                    mm(psb[bk][:, k * NCH:(k + 1) * NCH], TABB[:, g3 + k, :], U8[:, g3 + k, :], start=False, stop=True)
                p0 = R.sig('pe'); R.wait('act', p0)
                cp('act', Y8[:, g3:g3 + ng, :], psb[bk][:, 0:ng * NCH].rearrange("p (a b) -> p a b", a=ng))
                bankfree[bk] = R.sig('act')
            zstate['tabfree'] = sig_last('pe'); zstate['u8free'] = sig_last('pe')
            a0 = sig_last('act'); R.wait('sp', a0)
            for i_ in range(8):
                y8free = dma('sp', Y_dram[g0:g0 + 64, :, i_, :].rearrange("g ch n -> ch g n"),
                             Y8[16 * i_:16 * i_ + 16, :, :], 'zy')
            zstate['zfree'] = [sig_last('dve'), sig_last('pool'), y8free, sig_last('pe')]
        d0 = sig_last('dve'); R.wait('pe', d0)
        R.wait('pe', bankfree[6])
        tr(psb[6][:, 0:128], PF, ident_f); tr(psb[6][:, 128:256], SF, ident_f)
        p0 = R.sig('pe'); R.wait('act', p0)
        R.wait('act', y8free)
        cp('act', s0n, psb[6][:, 0:128]); cp('act', s0n2, psb[6][:, 128:256])
        a0 = R.sig('act'); bankfree[6] = a0; R.wait('sp', a0)
        out_toks.append(dma('sp', sp_o, s0n, 'out'))
        out_toks.append(dma('sp', ss_o, s0n2, 'out'))
        y_spill_done = y8free
        barrier()
        MG = P0
        merged = reg(MG, [32, TM], BF16)
        yT = reg(MG, [16, TM], BF16)
        SMo = MG + 69632
        smt = reg(SMo, [16, TM], BF16)
        wbuf2 = wbuf
        DT = SMo + 34816
        sgD = [reg(DT, [408]), reg(DT + 1632, [408])]
        tD = [reg(DT + 3264, [408]), reg(DT + 4896, [408])]
        R.wait('sp', y_spill_done)
        tky = dma('sp', yT, Y_dram.rearrange("(kt g8) ch i n -> (g8 ch) kt (i n)", g8=8), 'dl')
        tks = dma('sp', smt, szs_dram.rearrange("k p t -> p k t"), 'dl2')
        R.wait('pe', tky); R.wait('dve', tks)
        w_glu_v = w_glu.rearrange("(kt p) c -> p kt c", p=128)
        IP = [(0, 3), (3, 6), (6, 8)]
        dcnt = [0]
        for c in range(8):
            wb, wtok, widx = ws.load([(0, 16, w_glu_v[:, :, c * 256:(c + 1) * 256]),
                                      (16, 16, w_glu_v[:, :, 2048 + c * 256:2048 + (c + 1) * 256])], bufs=wbuf2)
            R.wait('pe', wtok)
            for mb in range(2):
                fb = 2 * c + mb
                for p in range(6):
                    R.wait('pe', bankfree[p])
                for kt in range(16):
                    for p, (i0, i1) in enumerate(IP):
                        n = (i1 - i0) * NCH
                        rhs = yT[:, kt, i0 * NCH:i1 * NCH]
                        mm(psb[p][:, 0:n], wb[:, kt, mb * 128:(mb + 1) * 128], rhs, start=(kt == 0), stop=(kt == 15))
                        mm(psb[3 + p][:, 0:n], wb[:, 16 + kt, mb * 128:(mb + 1) * 128], rhs, start=(kt == 0), stop=(kt == 15))
                p0 = R.sig('pe'); R.wait('act', p0); R.wait('dve', p0)
                for p, (i0, i1) in enumerate(IP):
                    n = (i1 - i0) * NCH
                    sl = dcnt[0] % 2; dcnt[0] += 1
                    act(sgD[sl][:, 0:n], psb[3 + p][:, 0:n], AF.Sigmoid, bias=bglu_t[:, 16 + fb:17 + fb])
                    a0 = R.sig('act'); bankfree[3 + p] = a0
                    R.wait('dve', a0)
                    stt(tD[sl][:, 0:n], psb[p][:, 0:n], bglu_t[:, fb:fb + 1], sgD[sl][:, 0:n], ALU.add, ALU.mult)
                    bankfree[p] = R.sig('dve')
                    dstv = smt[:, fb, :].rearrange("p (n i) -> p i n", i=8)[:, i0:i1, :]
                    tt('dve', dstv, tD[sl][:, 0:n].rearrange("p (i n) -> p i n", n=NCH), dstv, ALU.mult)
                    R.wait('act', R.sig('dve'))
            ws.release(widx, sig_last('pe'))
        barrier()
        if stop == 'E':
            break
        gbuf = [[reg(DT + (sl * 2 + j) * 2176, [TM], BF16) for j in range(2)] for sl in range(2)]
        ET = DT + 8704
        tE = [reg(ET, [384]), reg(ET + 1536, [384])]
        uE = [reg(ET + 3072, [384]), reg(ET + 4608, [384])]
        assert ET + 6144 <= ARENA_BYTES
        w_pa_v = w_pa.rearrange("(kt p) c -> p kt c", p=128)
        w_ps_v = w_ps.rearrange("(kt p) c -> p kt c", p=128)
        gfree = [None, None]
        ecnt = [0]
        for c in range(16):
            wb, wtok, widx = ws.load([(0, 16, w_pa_v[:, :, c * 256:(c + 1) * 256]),
                                      (16, 16, w_ps_v[:, :, c * 256:(c + 1) * 256])], bufs=wbuf2)
            R.wait('pe', wtok)
            for mb in range(2):
                fb = 2 * c + mb
                gs_ = fb % 2
                R.wait('sp', gfree[gs_])
                tg1 = dma('sp', gbuf[gs_][0], gate_dram[fb], f'g{gs_}')
                tg1 = dma('sp', gbuf[gs_][1], gate_dram[32 + fb], f'g{gs_}')
                for p in range(6):
                    R.wait('pe', bankfree[p])
                for kt in range(16):
                    for p, (a, n) in enumerate(PIECES):
                        mm(psb[p][:, 0:n], wb[:, kt, mb * 128:(mb + 1) * 128], am[:, kt, a:a + n], start=(kt == 0), stop=(kt == 15))
                        mm(psb[3 + p][:, 0:n], wb[:, 16 + kt, mb * 128:(mb + 1) * 128], smt[:, kt, a:a + n], start=(kt == 0), stop=(kt == 15))
                p0 = R.sig('pe'); R.wait('dve', p0); R.wait('dve', tg1)
                for p, (a, n) in enumerate(PIECES):
                    sl = ecnt[0] % 2; ecnt[0] += 1
                    tt('dve', tE[sl][:, 0:n], psb[p][:, 0:n], gbuf[gs_][0][:, a:a + n], ALU.mult)
                    bankfree[p] = R.sig('dve')
                    tt('dve', uE[sl][:, 0:n], psb[3 + p][:, 0:n], gbuf[gs_][1][:, a:a + n], ALU.mult)
                    d0 = R.sig('dve'); bankfree[3 + p] = d0
                    R.wait('pool', d0)
                    tt('pool', merged[:, fb, a:a + n], tE[sl][:, 0:n], uE[sl][:, 0:n], ALU.add)
                    R.wait('dve', R.sig('pool'))
                gfree[gs_] = sig_last('dve')
            ws.release(widx, sig_last('pe'))
        barrier()
        if stop == 'F':
            break
        wbuf3 = wbuf
        OP = SMo
        outp = [reg(OP + i * 16384, [D]) for i in range(4)] + [reg(AM + 16384, [D])]
        fg_rep = reg(AM, [D])
        ssF = reg(OP + 65536, [32]); rsF = reg(OP + 65664, [16])
        jF = reg(OP + 65728, [2048], BF16)
        assert OP + 65728 + 4096 <= ARENA_BYTES
        tfg = dma('sp', fg_rep, final_g.partition_broadcast(128), 'xa')
        w_out_v = w_out.rearrange("(kt p) c -> p kt c", p=128)
        ystore = None
        fcnt = [0]
        for half in range(2):
            tiles = list(range(0, 4)) if half == 0 else list(range(4, 9))
            R.wait('sp', ystore)
            xl = None
            for ti, t in enumerate(tiles):
                rows = 64 if t == 8 else 128
                xl = dma('sp', outp[ti][0:rows, :], xs[128 + t * 128:128 + t * 128 + rows, :], 'xf')
            R.wait('dve', xl)
            for c in range(16):
                wb, wtok, widx = ws.load([(0, 32, w_out_v[:, :, c * 256:(c + 1) * 256])], bufs=wbuf3)
                R.wait('pe', wtok)
                for ti, t in enumerate(tiles):
                    rows = 64 if t == 8 else 128
                    bk = fcnt[0] % 8; fcnt[0] += 1
                    R.wait('pe', bankfree[bk])
                    for kt in range(32):
                        mm(psb[bk][0:rows, 0:256], merged[:, kt, t * 128:t * 128 + rows], wb[:, kt, :],
                           start=(kt == 0), stop=(kt == 31))
                    p0 = R.sig('pe'); R.wait('dve', p0)
                    dst = outp[ti][0:rows, c * 256:(c + 1) * 256]
                    tt('dve', dst, psb[bk][0:rows, 0:256], dst, ALU.add)
                    bankfree[bk] = R.sig('dve')
                ws.release(widx, sig_last('pe'))
            R.wait('dve', tfg)
            for ti, t in enumerate(tiles):
                rows = 64 if t == 8 else 128
                R.wait('act', sig_last('dve'))
                act(jF[0:rows, :], outp[ti][0:rows, 0:2048], AF.Square, accum=ssF[0:rows, 2 * t:2 * t + 1])
                act(jF[0:rows, :], outp[ti][0:rows, 2048:4096], AF.Square, accum=ssF[0:rows, 2 * t + 1:2 * t + 2])
                a0 = R.sig('act'); R.wait('dve', a0)
                tt('dve', ssF[0:rows, 2 * t:2 * t + 1], ssF[0:rows, 2 * t:2 * t + 1], ssF[0:rows, 2 * t + 1:2 * t + 2], ALU.add)
                a0 = R.sig('act'); R.wait('dve', a0)
                ts('dve', rsF[0:rows, t:t + 1], ssF[0:rows, 2 * t:2 * t + 1], 1.0 / D, 1e-5, ALU.mult, ALU.add)
                d0 = R.sig('dve'); R.wait('act', d0)
                act(rsF[0:rows, t:t + 1], rsF[0:rows, t:t + 1], AF.Sqrt)
                a0 = R.sig('act'); R.wait('dve', a0)
                R.op('dve', lambda e, o_=rsF[0:rows, t:t + 1]: e.reciprocal(out=o_, in_=o_), r=[rsF[0:rows, t:t + 1]], w=[rsF[0:rows, t:t + 1]])
                stt(outp[ti][0:rows, :], outp[ti][0:rows, :], rsF[0:rows, t:t + 1], fg_rep[0:rows, :], ALU.mult, ALU.mult)
                d0 = R.sig('dve'); R.wait('sp', d0)
                ystore = dma('sp', y_o[t * 128:t * 128 + rows, :], outp[ti][0:rows, :], 'out')
                out_toks.append(ystore)

    if dbg is not None:
        barrier()
        R.wait('sp', sig_last('dve'), sig_last('act'), sig_last('pe'))
        out_toks.append(dma('sp', dbg_o, dbg[0](locals()), 'out'))
    if out_toks:
        R.wait('sp', out_toks[-1])
    for k_, v_ in R.dcnt.items():
        R.wait('sp', ('dma:' + k_, v_))
    sem_names = set()
    for e in Rec.ENG:
        for it in R.q[e]:
            if it[0] == 'wait':
                sem_names.add(it[1][0])
            elif it[0] == 'drain':
                pass
            elif it[3] is not None:
                sem_names.add('dma:' + it[3])
            elif it[2] is not None:
                sem_names.add(e)
    sems = {}
    for i, nm in enumerate(sorted(sem_names)):
        sems[nm] = es.enter_context(nc.semaphore(f"s{i}"))
    block = es.enter_context(nc.Block())
    engmap = {'pe': 'tensor', 'act': 'scalar', 'dve': 'vector', 'pool': 'gpsimd', 'sp': 'sync'}
    for e in Rec.ENG:
        def body(engobj, e=e):
            for it in R.q[e]:
                if it[0] == 'wait':
                    engobj.wait_ge(sems[it[1][0]], it[1][1])
                elif it[0] == 'drain':
                    engobj.drain()
                else:
                    ins = it[1](engobj)
                    if it[3] is not None:
                        ins.then_inc(sems['dma:' + it[3]], 16)
                    elif it[2] is not None:
                        ins.then_inc(sems[e], 1)
        getattr(block, engmap[e])(body)
    es.close()
    return nc


def _consts():
    d = np.arange(128)
    invf = (10000.0 ** (-np.arange(32, dtype=np.float32) / 32)).astype(np.float32)
    invf2 = invf[d % 32].reshape(128, 1).astype(np.float32)
    sgn = np.where((d % 64) < 32, -1.0, 1.0).astype(np.float32).reshape(128, 1)
    partner = np.where((d % 64) < 32, d + 32, d - 32)
    permT = np.zeros((128, 128), np.float32); permT[partner, d] = 1.0
    selE = np.zeros((128, 128), np.float32); selE[d % 64, d] = 1.0
    selO = np.zeros((128, 128), np.float32); selO[64 + d % 64, d] = 1.0
    s_ = d // 16; j_ = d % 16
    cmask = (s_[None, :] >= s_[:, None]).astype(np.float32)
    dmask = ((s_[None, :] == s_[:, None]) & (j_[None, :] == j_[:, None])).astype(np.float32)
    return dict(invf2=invf2, sgn=sgn, permT=permT, selE=selE, selO=selO, cmask=cmask, dmask=dmask,
                identf=np.eye(128, dtype=np.float32))


def make_in_maps(inp):
    f = lambda a: np.ascontiguousarray(np.asarray(a, dtype=np.float32))
    xp = f(inp['x_prompt']); xsamp = f(inp['x_sample'])
    cst = _consts()
    shared = dict(
        norm_g=f(inp['norm_g'][0]).reshape(1, D), final_g=f(inp['final_g']).reshape(1, D), w_in=f(inp['w_in'][0]),
        sink=f(inp['sink'][0]).reshape(1, 32), lre=f(inp['lambda_re'][0]), lim=f(inp['lambda_im'][0]),
        ldt=f(inp['log_dt'][0]).reshape(128, 1), bre=f(inp['b_re'][0]).reshape(128, 1024),
        bim=f(inp['b_im'][0]).reshape(128, 1024), cre=f(inp['c_re'][0]).reshape(128, 1024),
        cim=f(inp['c_im'][0]).reshape(128, 1024), dsk=f(inp['d_skip'][0]), w_glu=f(inp['w_glu'][0]),
        b_glu=f(inp['b_glu'][0]).reshape(4096, 1), w_pa=f(inp['w_pa'][0]), w_ps=f(inp['w_ps'][0]),
        w_out=f(inp['w_out'][0]), **cst)
    maps = []
    for c in range(NCORES):
        b, q = c // 4, c % 4
        xs = np.zeros((1216, D), np.float32)
        if q > 0:
            xs[0:128] = xp[b, q * 1024 - 128:q * 1024]
        xs[128:1152] = xp[b, q * 1024:(q + 1) * 1024]
        xs[1152:1216] = xsamp[c]
        pos = np.concatenate([np.arange(q * 1024 - 128, q * 1024), np.arange(q * 1024, (q + 1) * 1024),
                              np.arange(1024, 1088)]).astype(np.float32).reshape(1, 1216)
        pos = np.maximum(pos, 0.0)
        hb = NEG if q == 0 else 0.0
        bv = np.zeros((128, 8), np.float32)
        bv[0:64, 1] = NEG; bv[64:128, 2] = NEG; bv[:, 3] = hb; bv[0:64, 4] = NEG; bv[64:128, 4] = hb
        mc = np.zeros((8, 3), np.float32)
        for c2 in range(NCORES):
            k = q - 1 - (c2 % 4)
            if c2 // 4 == b and 0 <= k <= 2:
                mc[c2, k] = 1.0
        m = dict(shared)
        m.update(xs=xs, posv=pos, biasv=bv, mcoef=np.tile(mc.reshape(1, 24), (128, 1)).astype(np.float32),
                 ck=f(inp['cache_k'][0, c]).reshape(128, 256), cv=f(inp['cache_v'][0, c]).reshape(128, 256),
                 s0re=f(inp['state_ssm_re'][0, c]), s0im=f(inp['state_ssm_im'][0, c]))
        maps.append(m)
    return maps


def assemble(res):
    yp = np.zeros((2, 4096, D), np.float32); ysm = np.zeros((8, 64, D), np.float32)
    kp = np.zeros((1, 2, 128, 4, 64), np.float32); vp = np.zeros_like(kp)
    spr = np.zeros((1, 2, 128, 64), np.float32); spi = np.zeros_like(spr)
    ks = np.zeros((1, 8, 64, 4, 64), np.float32); vs = np.zeros_like(ks)
    ssr = np.zeros((1, 8, 128, 64), np.float32); ssi = np.zeros_like(ssr)
    for c in range(NCORES):
        r = res[c]; b, q = c // 4, c % 4
        yp[b, q * 1024:(q + 1) * 1024] = r['y'][0:1024]
        ysm[c] = r['y'][1024:1088]
        ks[0, c] = r['ks'].reshape(64, 4, 64); vs[0, c] = r['vs'].reshape(64, 4, 64)
        ssr[0, c] = r['ssms'][:, 0:64]; ssi[0, c] = r['ssms'][:, 64:128]
        if q == 3:
            kp[0, b] = r['kp'].reshape(128, 4, 64); vp[0, b] = r['vp'].reshape(128, 4, 64)
            spr[0, b] = r['ssmp'][:, 0:64]; spi[0, b] = r['ssmp'][:, 64:128]
    return (yp, ysm, kp, vp, spr, spi, ks, vs, ssr, ssi)


_NC = [None]


def kernel(**inputs):
    if _NC[0] is None:
        _NC[0] = build_program()
    maps = make_in_maps(inputs)
    res = run_bass_kernel_spmd(_NC[0], maps, core_ids=list(range(NCORES)))
    return assemble(res.results)
```
